# Optimizing a Trainium2 kernel written in Bass

```python
import math
import jax, jax.numpy as jnp
from jax import lax
import numpy as np

D_MODEL = 4096
BATCH = 1
SEQ = 16384
DEPTH = 2

N_MIXERS = 2
ATTN_HEADS = 16
ATTN_HEAD_DIM = D_MODEL // (2 * ATTN_HEADS)
ATTN_V_DIM = 2 * ATTN_HEAD_DIM
ATTN_WIDTH = ATTN_HEADS * ATTN_V_DIM
Q_BLOCK = 128
NUM_BUCKETS = 32
MAX_DISTANCE = 128
POOL_EXPAND = 2
POOL_WIDTH = POOL_EXPAND * D_MODEL
POOL_WINDOWS = (2, 4, 8, 16)
POOL_GROUPS = len(POOL_WINDOWS)
POOL_GROUP_DIM = POOL_WIDTH // POOL_GROUPS
N_ATTN_LAYERS = (DEPTH + 1) // 2
N_POOL_LAYERS = DEPTH // 2
NORM_EPS = 1e-6
SUBLN_EPS = 1e-5
NEG_INF = -1e30

kernel_name = 'hybrid_diffattn_multiscale_pool'


def rmsnorm(x, gain, eps):
    xf = x.astype(jnp.float32)
    inv = lax.rsqrt(jnp.mean(xf * xf, axis=-1, keepdims=True) + eps)
    return (xf * inv * gain.astype(jnp.float32)).astype(x.dtype)


def t5_causal_bucket(n):
    max_exact = NUM_BUCKETS // 2
    nf = jnp.maximum(n, max_exact).astype(jnp.float32)
    large = max_exact + (jnp.log(nf / max_exact) / math.log(MAX_DISTANCE / max_exact)
                         * (NUM_BUCKETS - max_exact)).astype(jnp.int32)
    large = jnp.minimum(large, NUM_BUCKETS - 1)
    return jnp.where(n < max_exact, n, large)


def lambda_init_fn(layer_idx):
    return 0.8 - 0.6 * math.exp(-0.3 * layer_idx)


def diff_attention_branch(h, w_in, lam_params, subln_gain, w_out, rel_bias, layer_idx):
    B, S, _ = h.shape
    n_blocks = S // Q_BLOCK
    proj = h @ w_in
    q, k, v, z = jnp.split(proj, 4, axis=-1)
    q = q.reshape(B, S, ATTN_HEADS, 2, ATTN_HEAD_DIM)
    k = k.reshape(B, S, ATTN_HEADS, 2, ATTN_HEAD_DIM)
    k1 = jnp.transpose(k[..., 0, :], (0, 2, 1, 3))
    k2 = jnp.transpose(k[..., 1, :], (0, 2, 1, 3))
    v = jnp.transpose(v.reshape(B, S, ATTN_HEADS, ATTN_V_DIM), (0, 2, 1, 3))

    def to_blocks(t):
        return jnp.transpose(t.reshape(B, n_blocks, Q_BLOCK, ATTN_HEADS, ATTN_HEAD_DIM), (1, 0, 3, 2, 4))

    q1b = to_blocks(q[..., 0, :])
    q2b = to_blocks(q[..., 1, :])

    lp = lam_params.astype(jnp.float32)
    lam_init = lambda_init_fn(layer_idx)
    lam = jnp.exp(jnp.sum(lp[0] * lp[1])) - jnp.exp(jnp.sum(lp[2] * lp[3])) + lam_init
    scale = ATTN_HEAD_DIM ** -0.5
    k_pos = jnp.arange(S)

    def block_fn(args):
        blk, qa, qb = args
        q_pos = blk * Q_BLOCK + jnp.arange(Q_BLOCK)
        dist = q_pos[:, None] - k_pos[None, :]
        causal = dist >= 0
        bias = jnp.transpose(rel_bias[t5_causal_bucket(jnp.maximum(dist, 0))].astype(jnp.float32), (2, 0, 1))

        def probs(qq, kk):
            logits = jnp.einsum('bhqd,bhkd->bhqk', qq, kk, preferred_element_type=jnp.float32) * scale + bias
            logits = jnp.where(causal, logits, NEG_INF)
            return jax.nn.softmax(logits, axis=-1)

        a = probs(qa, k1) - lam * probs(qb, k2)
        return jnp.einsum('bhqk,bhkd->bhqd', a.astype(v.dtype), v)

    o = lax.map(block_fn, (jnp.arange(n_blocks), q1b, q2b))
    o = jnp.transpose(o, (1, 0, 3, 2, 4)).reshape(B, S, ATTN_HEADS, ATTN_V_DIM)
    o = rmsnorm(o, subln_gain, SUBLN_EPS) * (1.0 - lam_init)
    o = o.reshape(B, S, ATTN_WIDTH) * jax.nn.silu(z)
    return o @ w_out


def pool_branch(h, w_in, w_group, scale, w_out):
    B, S, _ = h.shape
    u, z = jnp.split(h @ w_in, 2, axis=-1)
    ug = u.reshape(B, S, POOL_GROUPS, POOL_GROUP_DIM).astype(jnp.float32)
    cs = lax.cumsum(ug, axis=1)
    t = jnp.arange(S)
    pooled = []
    for g, w in enumerate(POOL_WINDOWS):
        c = cs[:, :, g]
        lag = jnp.pad(c, ((0, 0), (w, 0), (0, 0)))[:, :S]
        cnt = jnp.minimum(t + 1, w).astype(jnp.float32)[None, :, None]
        pooled.append((c - lag) / cnt - ug[:, :, g])
    p = jnp.stack(pooled, axis=2).astype(h.dtype)
    mixed = jnp.einsum('bsgc,gcd->bsgd', p, w_group).reshape(B, S, POOL_WIDTH) * scale
    return (mixed * jax.nn.silu(z)) @ w_out


def setup_inputs(seed: int = 0) -> dict:
    key = jax.random.key(seed)
    ks = jax.random.split(key, 13)
    f32 = jnp.float32
    x = jax.random.normal(ks[0], (BATCH, SEQ, D_MODEL), f32)
    norm_gains = 1.0 + 0.02 * jax.random.normal(ks[1], (DEPTH, D_MODEL), f32)
    final_norm_gain = 1.0 + 0.02 * jax.random.normal(ks[2], (D_MODEL,), f32)
    rel_bias = 0.2 * jax.random.normal(ks[3], (NUM_BUCKETS, ATTN_HEADS), f32)
    attn_w_in = jax.random.normal(ks[4], (N_ATTN_LAYERS, D_MODEL, 4 * ATTN_WIDTH), f32) * D_MODEL ** -0.5
    attn_lambda = 0.1 * jax.random.normal(ks[5], (N_ATTN_LAYERS, 4, ATTN_HEAD_DIM), f32)
    attn_subln_gain = 1.0 + 0.02 * jax.random.normal(ks[6], (N_ATTN_LAYERS, ATTN_V_DIM), f32)
    attn_w_out = jax.random.normal(ks[7], (N_ATTN_LAYERS, ATTN_WIDTH, D_MODEL), f32) * ATTN_WIDTH ** -0.5
    pool_w_in = jax.random.normal(ks[8], (N_POOL_LAYERS, D_MODEL, 2 * POOL_WIDTH), f32) * D_MODEL ** -0.5
    pool_w_group = jax.random.normal(ks[9], (N_POOL_LAYERS, POOL_GROUPS, POOL_GROUP_DIM, POOL_GROUP_DIM), f32) * POOL_GROUP_DIM ** -0.5
    pool_scale = 1.0 + 0.02 * jax.random.normal(ks[10], (N_POOL_LAYERS, POOL_WIDTH), f32)
    pool_w_out = jax.random.normal(ks[11], (N_POOL_LAYERS, POOL_WIDTH, D_MODEL), f32) * POOL_WIDTH ** -0.5
    return {'x': x, 'norm_gains': norm_gains, 'final_norm_gain': final_norm_gain, 'rel_bias': rel_bias,
            'attn_w_in': attn_w_in, 'attn_lambda': attn_lambda, 'attn_subln_gain': attn_subln_gain,
            'attn_w_out': attn_w_out, 'pool_w_in': pool_w_in, 'pool_w_group': pool_w_group,
            'pool_scale': pool_scale, 'pool_w_out': pool_w_out}


def reference(x, norm_gains, final_norm_gain, rel_bias, attn_w_in, attn_lambda, attn_subln_gain,
              attn_w_out, pool_w_in, pool_w_group, pool_scale, pool_w_out):
    for i in range(DEPTH):
        h = rmsnorm(x, norm_gains[i], NORM_EPS)
        j = i // N_MIXERS
        if i % N_MIXERS == 0:
            y = diff_attention_branch(h, attn_w_in[j], attn_lambda[j], attn_subln_gain[j],
                                      attn_w_out[j], rel_bias, i)
        else:
            y = pool_branch(h, pool_w_in[j], pool_w_group[j], pool_scale[j], pool_w_out[j])
        x = x + y
    return rmsnorm(x, final_norm_gain, NORM_EPS)
```

```python
import math
from contextlib import ExitStack

import numpy as np
import ml_dtypes

import concourse.bass as bass
import concourse.mybir as mybir
from concourse.bass_utils import run_bass_kernel_spmd

F32 = mybir.dt.float32
BF16 = mybir.dt.bfloat16
AF = mybir.ActivationFunctionType
ALU = mybir.AluOpType
AX = mybir.AxisListType

D = 4096
NCH = 32
SEQ = 16384
NCORES = 8
HD = 128
DV = 256
NORM_EPS = 1e-6
SUBLN_EPS = 1e-5
SCALE = HD ** -0.5
LAM_INIT0 = 0.8 - 0.6 * math.exp(-0.3 * 0)
PW = 8192
GDIM = 2048
WINDOWS = (2, 4, 8, 16)
ENGS = ("sync", "scalar", "gpsimd", "vector", "tensor")


class Stage:
    def __init__(self, nc, es, name):
        self.nc = nc
        self.es = es
        self.name = name
        self.q = {e: [] for e in ENGS}
        self.psem = {}
        self.cnt = {}
        for e in ("scalar", "gpsimd", "vector", "tensor"):
            self.psem[e] = es.enter_context(nc.semaphore(f"{name}_p_{e}"))
            self.cnt[e] = 0
        self.nsem = 0
        self.out_tokens = []

    def sbuf(self, name, shape, dt):
        return self.es.enter_context(self.nc.sbuf_tensor(f"{self.name}_{name}", list(shape), dt))

    def psum(self, name, shape, dt):
        return self.es.enter_context(self.nc.psum_tensor(f"{self.name}_{name}", list(shape), dt))

    def dsem(self):
        self.nsem += 1
        return [self.es.enter_context(self.nc.semaphore(f"{self.name}_d{self.nsem}")), 0]

    def op(self, eng, fn, waits=(), ms=False, force=()):
        tok = None
        if ms:
            self.cnt[eng] += 1
            tok = (self.psem[eng], self.cnt[eng])
        self.q[eng].append((fn, [w for w in waits if w is not None], ("ms", None) if ms else None,
                            [w for w in force if w is not None]))
        return tok

    def dma(self, eng, out, in_, sem, waits=()):
        sem[1] += 16
        tok = (sem[0], sem[1])
        self.q[eng].append((lambda e: e.dma_start(out=out, in_=in_), [w for w in waits if w is not None],
                            ("dma", sem[0]), []))
        return tok

    def wait_only(self, eng, waits):
        self.q[eng].append((None, [w for w in waits if w is not None], None, []))

    def run(self):
        nc = self.nc
        with nc.Block() as block:
            def body(ename):
                def f(eng):
                    seen = {}
                    for fn, waits, inc, force in self.q[ename]:
                        done = set()
                        for (s, v) in force:
                            if (id(s), v) in done:
                                continue
                            done.add((id(s), v))
                            seen[id(s)] = max(seen.get(id(s), 0), v)
                            eng.wait_ge(s, v)
                        for (s, v) in waits:
                            k = id(s)
                            if seen.get(k, 0) >= v:
                                continue
                            seen[k] = v
                            eng.wait_ge(s, v)
                        if fn is None:
                            continue
                        ins = fn(eng)
                        if inc is not None:
                            if inc[0] == "ms":
                                ins.then_inc(self.psem[ename], 1)
                            else:
                                ins.then_inc(inc[1], 16)
                return f
            block.sync(body("sync"))
            block.scalar(body("scalar"))
            block.gpsimd(body("gpsimd"))
            block.vector(body("vector"))
            block.tensor(body("tensor"))


def run_stage(nc, name, builder):
    with ExitStack() as es:
        st = Stage(nc, es, name)
        builder(st)
        st.run()


def stage_normT(nc, name, x_d, gain_d, ident_d, hT_d, nblk):
    def build(st):
        xb = [st.sbuf(f"x{i}", [128, D], F32) for i in range(2)]
        gb = st.sbuf("gb", [128, D], F32)
        hn = [st.sbuf(f"hn{i}", [128, D], BF16) for i in range(2)]
        ht = [st.sbuf(f"ht{i}", [128, NCH, 128], BF16) for i in range(2)]
        ss = st.sbuf("ss", [128, 2], F32)
        rs = st.sbuf("rs", [128, 2], F32)
        epsb = st.sbuf("epsb", [128, 1], F32)
        t_eps = st.op("vector", lambda e: e.memset(epsb[:], D * NORM_EPS), ms=True)
        ident = st.sbuf("ident", [128, 128], BF16)
        tp = [[st.psum(f"tp{s}_{g}", [128, 8, 128], BF16) for g in range(4)] for s in range(2)]
        xs = [st.dsem() for _ in range(2)]
        hs = [st.dsem() for _ in range(2)]
        cs = st.dsem()
        t_id = st.dma("gpsimd", ident[:], ident_d, cs)
        t_gb0 = st.dma("gpsimd", gb[:], bass.AP(gain_d.tensor, gain_d.offset, [[0, 128], [1, D]]), cs)
        t_gb = st.op("vector", lambda e: e.tensor_scalar(out=gb[:], in0=gb[:], scalar1=float(D ** 0.5), scalar2=0.0,
                                                         op0=ALU.mult, op1=ALU.add), waits=[t_gb0], ms=True)
        tH = [None] * nblk
        tT = [None] * nblk
        tEv = [None] * nblk
        tEa = [None] * nblk
        tOut = [None] * nblk

        def norm(i):
            s = i % 2
            t_x = st.dma("sync", xb[s][:], x_d[i * 128:(i + 1) * 128, :], xs[s],
                         waits=[tH[i - 2] if i >= 2 else None])
            t0 = st.op("vector", lambda e: e.memset(ss[:, s:s + 1], 0.0), ms=True,
                       waits=[tH[i - 2] if i >= 2 else None])
            tA = st.op("scalar", lambda e: e.activation(out=hn[s][:], in_=xb[s][:], func=AF.Square,
                                                        accum_out=ss[:, s:s + 1]),
                       waits=[t_x, t0, tT[i - 2] if i >= 2 else None], ms=True)
            t1 = st.op("scalar", lambda e: e.activation(out=rs[:, s:s + 1], in_=ss[:, s:s + 1], func=AF.Sqrt,
                                                        bias=epsb[:, 0:1], scale=1.0), waits=[tA, t_eps], ms=True)
            t2 = st.op("vector", lambda e: e.reciprocal(out=rs[:, s:s + 1], in_=rs[:, s:s + 1]),
                       waits=[t1], ms=True)
            tH[i] = st.op("vector", lambda e: e.scalar_tensor_tensor(out=hn[s][:], in0=xb[s][:],
                                                                     scalar=rs[:, s:s + 1], in1=gb[:],
                                                                     op0=ALU.mult, op1=ALU.mult),
                          waits=[t2, t_gb], ms=True)
            for c in range(NCH):
                g, j = divmod(c, 8)
                last = c == NCH - 1
                w = []
                if c == 0:
                    w = [tH[i], t_id, tEv[i - 2] if i >= 2 else None, tEa[i - 2] if i >= 2 else None]
                tk = st.op("tensor", (lambda e, g=g, j=j, c=c: e.transpose(
                    out=tp[s][g][:, j, :], in_=hn[s][:, c * 128:(c + 1) * 128], identity=ident[:])),
                    waits=w, ms=last)
                if last:
                    tT[i] = tk

        def evac(i):
            s = i % 2
            wfree = tOut[i - 2] if i >= 2 else None
            for g in range(4):
                eng = "vector" if g < 2 else "scalar"
                if eng == "vector":
                    tk = st.op(eng, (lambda e, g=g: e.tensor_copy(out=ht[s][:, g * 8:(g + 1) * 8, :],
                                                                 in_=tp[s][g][:])),
                               waits=[tT[i], wfree], ms=(g == 1))
                    if g == 1:
                        tEv[i] = tk
                else:
                    tk = st.op(eng, (lambda e, g=g: e.activation(out=ht[s][:, g * 8:(g + 1) * 8, :],
                                                                in_=tp[s][g][:], func=AF.Copy)),
                               waits=[tT[i], wfree], ms=(g == 3))
                    if g == 3:
                        tEa[i] = tk
            tOut[i] = st.dma("sync", hT_d[i], ht[s][:], hs[s], waits=[tEv[i], tEa[i]])

        for i in range(nblk + 1):
            if i < nblk:
                norm(i)
            if i >= 1:
                evac(i - 1)
        st.wait_only("sync", [tOut[nblk - 1], tOut[nblk - 2] if nblk >= 2 else None])
    run_stage(nc, name, build)


def stage_proj(nc, name, hT_d, w_d, qkT_d, vz_d, nblk):
    ntile = nblk // 4

    def build(st):
        wbf = st.sbuf("w", [128, NCH, 1024], BF16)
        hTt = [st.sbuf(f"h{i}", [128, 4, NCH, 128], BF16) for i in range(2)]
        stq = [st.sbuf(f"sq{i}", [128, 4, 128], BF16) for i in range(2)]
        stv = [st.sbuf(f"sv{i}", [128, 512], BF16) for i in range(2)]
        accq = [st.psum(f"aq{i}", [128, 4, 128], F32) for i in range(2)]
        accv = [st.psum(f"av{i}", [128, 512], F32) for i in range(2)]
        ws = st.dsem()
        hsem = [st.dsem() for _ in range(2)]
        oq = [st.dsem() for _ in range(2)]
        ov = [st.dsem() for _ in range(2)]
        t_w = None
        for k in range(4):
            t_w = st.dma("gpsimd", wbf[:, k * 8:(k + 1) * 8, :], w_d[:, k * 8:(k + 1) * 8, :], ws)
        nq = 0
        nv = 0
        tq_ev = {}
        tv_ev = {}
        tq_out = {}
        tv_out = {}
        t_last_mm = [None] * ntile
        for tt in range(ntile):
            s = tt % 2
            t_h = st.dma("sync", hTt[s][:], hT_d[4 * tt:4 * tt + 4].rearrange("b p c t -> p b c t"), hsem[s],
                         waits=[t_last_mm[tt - 2] if tt >= 2 else None])
            for cc in range(4):
                a = nq % 2
                tk = None
                for c in range(NCH):
                    w = []
                    if c == 0:
                        w = [t_h, t_w, tq_ev.get(nq - 2)]
                    tk = st.op("tensor", (lambda e, a=a, c=c, cc=cc, s=s: e.matmul(
                        accq[a][:], lhsT=wbf[:, c, cc * 128:(cc + 1) * 128], rhs=hTt[s][:, :, c, :],
                        start=(c == 0), stop=(c == NCH - 1))), waits=w, ms=(c == NCH - 1))
                eng = "vector" if nq % 2 == 0 else "scalar"
                if eng == "vector":
                    tq_ev[nq] = st.op(eng, (lambda e, a=a: e.tensor_copy(out=stq[a][:], in_=accq[a][:])),
                                      waits=[tk, tq_out.get(nq - 2)], ms=True)
                else:
                    tq_ev[nq] = st.op(eng, (lambda e, a=a: e.activation(out=stq[a][:], in_=accq[a][:],
                                                                        func=AF.Copy)),
                                      waits=[tk, tq_out.get(nq - 2)], ms=True)
                tq_out[nq] = st.dma("sync", qkT_d[cc, :, tt * 512:(tt + 1) * 512],
                                    stq[a][:].rearrange("p a b -> p (a b)"), oq[a], waits=[tq_ev[nq]])
                nq += 1
            for b in range(4):
                a = nv % 2
                tk = None
                for c in range(NCH):
                    w = []
                    if c == 0:
                        w = [tv_ev.get(nv - 2)]
                    tk = st.op("tensor", (lambda e, a=a, c=c, b=b, s=s: e.matmul(
                        accv[a][:], lhsT=hTt[s][:, b, c, :], rhs=wbf[:, c, 512:1024],
                        start=(c == 0), stop=(c == NCH - 1))), waits=w, ms=(c == NCH - 1))
                eng = "scalar" if nv % 2 == 0 else "vector"
                if eng == "vector":
                    tv_ev[nv] = st.op(eng, (lambda e, a=a: e.tensor_copy(out=stv[a][:], in_=accv[a][:])),
                                      waits=[tk, tv_out.get(nv - 2)], ms=True)
                else:
                    tv_ev[nv] = st.op(eng, (lambda e, a=a: e.activation(out=stv[a][:], in_=accv[a][:],
                                                                        func=AF.Copy)),
                                      waits=[tk, tv_out.get(nv - 2)], ms=True)
                blk = 4 * tt + b
                tv_out[nv] = st.dma("sync", vz_d[blk * 128:(blk + 1) * 128, :], stv[a][:], ov[a],
                                    waits=[tv_ev[nv]])
                t_last_mm[tt] = tk
                nv += 1
        st.wait_only("sync", [tq_out[nq - 1], tq_out[nq - 2], tv_out[nv - 1], tv_out[nv - 2]])
    run_stage(nc, name, build)


def stage_attn(nc, name, qkT_d, vz_d, tdiag_d, tadj_d, mask_d, c31_d, lam_d, sg_d, ident_d, ogT_d, nblk):
    S = nblk * 128
    NQ = nblk // 2
    LOOK = 2
    NSB = 3
    NPT = 4

    def build(st):
        kT = st.sbuf("kT", [128, 2, S], BF16)
        vaug = st.sbuf("vaug", [128, nblk, 258], BF16)
        qT = [st.sbuf(f"qT{i}", [128, 2, 256], BF16) for i in range(3)]
        zt = [st.sbuf(f"zt{i}", [128, 2, 256], BF16) for i in range(3)]
        pt = [st.sbuf(f"pt{i}", [128, 2, 256], BF16) for i in range(NPT)]
        identb = st.sbuf("identb", [128, 128], BF16)
        identf = st.sbuf("identf", [128, 128], F32)
        tdg = st.sbuf("tdg", [128, 128], F32)
        tad = st.sbuf("tad", [128, 128], F32)
        msk = st.sbuf("msk", [128, 128], F32)
        c31 = st.sbuf("c31", [128, 1], F32)
        epsb = st.sbuf("epsb", [128, 1], F32)
        lamb = st.sbuf("lamb", [128, 512], F32)
        lamt = st.sbuf("lamt", [128, 256], F32)
        lams = st.sbuf("lams", [128, 4], F32)
        nlam = st.sbuf("nlam", [128, 1], F32)
        sg2 = st.sbuf("sg2", [128, 256], F32)
        rr = st.sbuf("rr", [128, 8], F32)
        t1 = [st.sbuf(f"t1_{i}", [128, 256], F32) for i in range(2)]
        ob = [st.sbuf(f"ob_{i}", [128, 256], F32) for i in range(2)]
        junk = st.sbuf("junk", [128, 256], F32)
        ez = [[st.sbuf(f"ez_{a}{i}", [128, 256], F32) for i in range(2)] for a in range(2)]
        gz = [[st.sbuf(f"gz_{a}{i}", [128, 256], F32) for i in range(2)] for a in range(2)]
        og = [st.sbuf(f"og_{i}", [128, 256], BF16) for i in range(2)]
        ogTs = [st.sbuf(f"ogT_{i}", [128, 2, 256], BF16) for i in range(2)]
        sb = [st.psum(f"sb{i}", [128, 2, 256], F32) for i in range(NSB)]
        O = [[st.psum(f"O{p}{s}", [128, 512], F32) for s in range(2)] for p in range(2)]
        tpo = st.psum("tpo", [128, 2, 2, 128], BF16)

        cs = st.dsem()
        ks = st.dsem()
        vs = st.dsem()
        qs = [st.dsem() for _ in range(3)]
        zs = [st.dsem() for _ in range(3)]
        os_ = [st.dsem() for _ in range(2)]

        st.dma("gpsimd", identb[:], ident_d, cs)
        st.dma("sync", identf[:], ident_d, cs)
        st.dma("sync", tdg[:], tdiag_d, cs)
        st.dma("sync", tad[:], tadj_d, cs)
        st.dma("sync", msk[:], mask_d, cs)
        st.dma("sync", c31[:], bass.AP(c31_d.tensor, c31_d.offset, [[0, 128], [1, 1]]), cs)
        st.dma("sync", lamb[:], bass.AP(lam_d.tensor, lam_d.offset, [[0, 128], [1, 512]]), cs)
        t_c = st.dma("sync", sg2[:], bass.AP(sg_d.tensor, sg_d.offset, [[0, 128], [1, 256]]), cs)
        a1 = st.op("vector", lambda e: e.tensor_scalar(out=tdg[:], in0=tdg[:], scalar1=c31[:, 0:1],
                                                       scalar2=1.0 / SCALE, op0=ALU.subtract, op1=ALU.mult),
                   waits=[t_c], ms=True)
        a2 = st.op("vector", lambda e: e.tensor_tensor(out=tdg[:], in0=tdg[:], in1=msk[:], op=ALU.add),
                   waits=[a1], ms=True)
        a3 = st.op("vector", lambda e: e.tensor_scalar(out=tad[:], in0=tad[:], scalar1=c31[:, 0:1],
                                                       scalar2=1.0 / SCALE, op0=ALU.subtract, op1=ALU.mult),
                   waits=[a2], ms=True)
        a4 = st.op("vector", lambda e: e.tensor_tensor(
            out=lamt[:].rearrange("p (a b) -> p a b", a=2),
            in0=lamb[:].rearrange("p (a two b) -> p a two b", a=2, two=2)[:, :, 0, :],
            in1=lamb[:].rearrange("p (a two b) -> p a two b", a=2, two=2)[:, :, 1, :], op=ALU.mult),
            waits=[a3], ms=True)
        a5 = st.op("vector", lambda e: e.tensor_reduce(out=lams[:, 0:2],
                                                       in_=lamt[:].rearrange("p (a b) -> p a b", a=2),
                                                       axis=AX.X, op=ALU.add), waits=[a4], ms=True)
        a6 = st.op("scalar", lambda e: e.activation(out=lams[:, 2:4], in_=lams[:, 0:2], func=AF.Exp),
                   waits=[a5], ms=True)
        a7 = st.op("vector", lambda e: e.tensor_tensor(out=nlam[:], in0=lams[:, 3:4], in1=lams[:, 2:3],
                                                       op=ALU.subtract), waits=[a6], ms=True)
        a8 = st.op("vector", lambda e: e.tensor_scalar(out=nlam[:], in0=nlam[:], scalar1=-LAM_INIT0,
                                                       scalar2=1.0, op0=ALU.add, op1=ALU.mult), waits=[a7], ms=True)
        a9 = st.op("vector", lambda e: e.tensor_scalar(out=sg2[:], in0=sg2[:], scalar1=(1.0 - LAM_INIT0) * 16.0,
                                                       scalar2=0.0, op0=ALU.mult, op1=ALU.add), waits=[a8], ms=True)
        a10 = st.op("vector", lambda e: e.memset(vaug[:, :, 256:258], 1.0), waits=[a9], ms=True)
        a11 = st.op("vector", lambda e: e.memset(epsb[:], DV * SUBLN_EPS), waits=[a10], ms=True)
        t_const = a11
        t_k = None
        for c_ in range(2):
            t_k = st.dma("sync", kT[:, c_, :], qkT_d[2 + c_], ks)
        t_v = None
        for k in range(nblk):
            t_v = st.dma("sync", vaug[:, k, 0:256], vz_d[k * 128:(k + 1) * 128, 0:256], vs)

        steps = [(j, kb) for j in range(NQ) for kb in range(2 * j + 2)]
        nsteps = len(steps)
        t_qk = [None] * nsteps
        t_exp = [None] * nsteps
        t_pv = [None] * nsteps
        t_q = [None] * NQ
        t_z = [None] * NQ
        t_fin_O = [None] * NQ
        t_og = [[None, None] for _ in range(NQ)]
        t_tr = [None] * NQ
        t_trev = [None] * NQ
        t_out = [None] * NQ
        t_gate = [[None, None] for _ in range(NQ)]
        last_step_of = {}
        for i, (j, kb) in enumerate(steps):
            last_step_of[j] = i

        def load_q(j):
            s = j % 3
            w = [t_qk[last_step_of[j - 3]]] if j >= 3 else []
            for c_ in range(2):
                t_q[j] = st.dma("sync", qT[s][:, c_, :], qkT_d[c_, :, j * 256:(j + 1) * 256], qs[s], waits=w)
            s2 = j % 3
            w = [t_gate[j - 3][0], t_gate[j - 3][1]] if j >= 3 else []
            for c_ in range(2):
                t_z[j] = st.dma("sync", zt[s2][:, c_, :],
                                vz_d[j * 256 + c_ * 128:j * 256 + (c_ + 1) * 128, 256:512], zs[s2], waits=w)

        def emit_qk(i):
            j, kb = steps[i]
            bank = sb[i % NSB]
            q = qT[j % 3]
            w0 = [t_q[j], t_k, t_const, t_exp[i - NSB] if i >= NSB else None]
            specials = {}
            if kb == 2 * j + 1:
                valid = [1]
                specials[1] = tdg
            elif kb == 2 * j:
                valid = [0, 1]
                specials[0] = tdg
                specials[1] = tad
            elif kb == 2 * j - 1:
                valid = [0, 1]
                specials[0] = tad
            else:
                valid = [0, 1]
            tk = None
            first = True
            if not specials:
                for p in range(2):
                    tk = st.op("tensor", (lambda e, p=p: e.matmul(
                        bank[:, p, :], lhsT=kT[:, p, kb * 128:(kb + 1) * 128], rhs=q[:, p, :],
                        start=True, stop=True)), waits=w0 if first else [], ms=(p == 1), force=[t_k])
                    first = False
            else:
                n = 0
                tot = 2 * len(valid)
                for p in range(2):
                    for s in valid:
                        n += 1
                        if s in specials:
                            tt_ = specials[s]
                            st.op("tensor", (lambda e, p=p, s=s, tt_=tt_: e.matmul(
                                bank[:, p, s * 128:(s + 1) * 128], lhsT=identf[:], rhs=tt_[:],
                                start=True, stop=False)), waits=w0 if first else [], force=[t_const])
                            first = False
                            tk = st.op("tensor", (lambda e, p=p, s=s: e.matmul(
                                bank[:, p, s * 128:(s + 1) * 128], lhsT=kT[:, p, kb * 128:(kb + 1) * 128],
                                rhs=q[:, p, s * 128:(s + 1) * 128], start=False, stop=True)),
                                ms=(n == tot), force=[t_k, t_q[j]])
                        else:
                            tk = st.op("tensor", (lambda e, p=p, s=s: e.matmul(
                                bank[:, p, s * 128:(s + 1) * 128], lhsT=kT[:, p, kb * 128:(kb + 1) * 128],
                                rhs=q[:, p, s * 128:(s + 1) * 128], start=True, stop=True)),
                                waits=w0 if first else [], ms=(n == tot), force=[t_k])
                            first = False
            t_qk[i] = tk
            c0 = valid[0] * 128
            pslot = pt[i % NPT]
            t_exp[i] = st.op("scalar", (lambda e: e.activation(
                out=pslot[:, :, c0:256], in_=bank[:, :, c0:256], func=AF.Exp, bias=c31[:, 0:1], scale=SCALE)),
                waits=[t_qk[i], t_pv[i - NPT] if i >= NPT else None], ms=True)

        def emit_pv(i):
            j, kb = steps[i]
            valid = [1] if kb == 2 * j + 1 else [0, 1]
            pslot = pt[i % NPT]
            first = True
            tk = None
            n = 0
            for p in range(2):
                for s in valid:
                    n += 1
                    lastkb = 2 * j if s == 0 else 2 * j + 1
                    w = []
                    if first:
                        w = [t_exp[i], t_v]
                        if kb == 0 and j >= 1:
                            w.append(t_fin_O[j - 1])
                        first = False
                    tk = st.op("tensor", (lambda e, p=p, s=s, lastkb=lastkb: e.matmul(
                        O[p][s][:, 0:257], lhsT=pslot[:, p, s * 128:(s + 1) * 128], rhs=vaug[:, kb, 0:257],
                        start=(kb == 0), stop=(kb == lastkb))), waits=w, ms=(n == 2 * len(valid)),
                        force=[t_exp[i]])
            t_pv[i] = tk

        def emit_gate(j):
            z = zt[j % 3]
            a = j % 2
            for s in range(2):
                wprev = [t_og[j - 2][s]] if j >= 2 else []
                g0 = st.op("scalar", (lambda e, s=s: e.activation(out=ez[a][s][:], in_=z[:, s, :], func=AF.Exp,
                                                                   scale=-1.0)),
                           waits=[t_z[j]] + wprev, ms=True)
                g1 = st.op("gpsimd", (lambda e, s=s: e.tensor_scalar(out=ez[a][s][:], in0=ez[a][s][:], scalar1=1.0,
                                                                     scalar2=1.0, op0=ALU.add, op1=ALU.mult)),
                           waits=[g0], ms=True)
                g1 = st.op("vector", (lambda e, s=s: e.reciprocal(out=ez[a][s][:], in_=ez[a][s][:])),
                           waits=[g1], ms=True)
                g2 = st.op("gpsimd", (lambda e, s=s: e.tensor_tensor(out=gz[a][s][:], in0=z[:, s, :], in1=ez[a][s][:],
                                                                     op=ALU.mult)),
                           waits=[g1] + wprev, ms=True)
                g3 = st.op("gpsimd", (lambda e, s=s: e.tensor_tensor(out=gz[a][s][:], in0=gz[a][s][:], in1=sg2[:],
                                                                     op=ALU.mult)),
                           waits=[g2, t_const], ms=True)
                t_gate[j][s] = g3

        def emit_fin(j):
            ilast = last_step_of[j]
            prev = t_pv[ilast]
            for s in range(2):
                c = 4 * s
                f1 = st.op("vector", (lambda e, s=s, c=c: e.reciprocal(out=rr[:, c:c + 1], in_=O[0][s][:, 256:257])),
                           waits=[prev, t_og[j - 1][s] if j >= 1 else None], ms=True)
                f2 = st.op("vector", (lambda e, s=s, c=c: e.reciprocal(out=rr[:, c + 1:c + 2],
                                                                        in_=O[1][s][:, 256:257])),
                           waits=[f1], ms=True)
                f3 = st.op("vector", (lambda e, c=c: e.tensor_tensor(out=rr[:, c + 1:c + 2], in0=rr[:, c + 1:c + 2],
                                                                     in1=nlam[:], op=ALU.mult)),
                           waits=[f2], ms=True)
                f4 = st.op("vector", (lambda e, s=s, c=c: e.tensor_scalar(out=t1[s][:], in0=O[0][s][:, 0:256],
                                                                          scalar1=rr[:, c:c + 1], scalar2=0.0,
                                                                          op0=ALU.mult, op1=ALU.add)),
                           waits=[f3], ms=True)
                f5 = st.op("vector", (lambda e, s=s, c=c: e.scalar_tensor_tensor(
                    out=ob[s][:], in0=O[1][s][:, 0:256], scalar=rr[:, c + 1:c + 2], in1=t1[s][:],
                    op0=ALU.mult, op1=ALU.add)), waits=[f4], ms=True)
                prev = f5
                if s == 1:
                    t_fin_O[j] = f5
            for s in range(2):
                c = 4 * s
                f6 = st.op("vector", (lambda e, s=s: e.tensor_tensor(out=junk[:], in0=ob[s][:], in1=ob[s][:],
                                                                     op=ALU.mult)), waits=[prev], ms=True)
                f7 = st.op("vector", (lambda e, c=c: e.tensor_reduce(out=rr[:, c + 2:c + 3], in_=junk[:],
                                                                     axis=AX.X, op=ALU.add)), waits=[f6], ms=True)
                f8 = st.op("scalar", (lambda e, c=c: e.activation(out=rr[:, c + 3:c + 4], in_=rr[:, c + 2:c + 3],
                                                                  func=AF.Ln, bias=epsb[:, 0:1], scale=1.0)),
                           waits=[f7], ms=True)
                f9 = st.op("scalar", (lambda e, c=c: e.activation(out=rr[:, c + 3:c + 4], in_=rr[:, c + 3:c + 4],
                                                                  func=AF.Exp, scale=-0.5)),
                           waits=[f8], ms=True)
                f10 = st.op("vector", (lambda e, s=s, c=c: e.scalar_tensor_tensor(
                    out=og[s][:], in0=ob[s][:], scalar=rr[:, c + 3:c + 4], in1=gz[j % 2][s][:],
                    op0=ALU.mult, op1=ALU.mult)),
                    waits=[f9, t_gate[j][s], t_tr[j - 1] if j >= 1 else None], ms=True)
                t_og[j][s] = f10
                prev = f10

        def emit_tr(j):
            tk = None
            n = 0
            for s in range(2):
                for dvc in range(2):
                    n += 1
                    w = []
                    if n == 1:
                        w = [t_og[j][0], t_og[j][1], t_trev[j - 1] if j >= 1 else None]
                    tk = st.op("tensor", (lambda e, s=s, dvc=dvc: e.transpose(
                        out=tpo[:, dvc, s, :], in_=og[s][:, dvc * 128:(dvc + 1) * 128], identity=identb[:])),
                        waits=w, ms=(n == 4), force=[t_og[j][s]])
            t_tr[j] = tk
            o = ogTs[j % 2]
            t_trev[j] = st.op("vector", (lambda e: e.tensor_copy(
                out=o[:].rearrange("p d (s q) -> p d s q", s=2), in_=tpo[:])),
                waits=[tk, t_out[j - 2] if j >= 2 else None], ms=True)
            for c_ in range(2):
                t_out[j] = st.dma("sync", ogT_d[c_, :, j * 256:(j + 1) * 256], o[:, c_, :],
                                  os_[j % 2], waits=[t_trev[j]])

        for j in range(min(2, NQ)):
            load_q(j)
        pending_tr = []
        for i in range(nsteps + LOOK):
            if i < nsteps:
                j, kb = steps[i]
                if kb == 0:
                    if j + 2 < NQ:
                        load_q(j + 2)
                    emit_gate(j)
                emit_qk(i)
            if i >= LOOK:
                ip = i - LOOK
                emit_pv(ip)
                j, kb = steps[ip]
                if ip == last_step_of[j]:
                    while pending_tr:
                        emit_tr(pending_tr.pop(0)[0])
                    emit_fin(j)
                    pending_tr.append([j, 6])
            for ent in pending_tr:
                ent[1] -= 1
            while pending_tr and pending_tr[0][1] <= 0:
                emit_tr(pending_tr.pop(0)[0])
        while pending_tr:
            emit_tr(pending_tr.pop(0)[0])
        st.wait_only("sync", [t_out[NQ - 1], t_out[NQ - 2] if NQ >= 2 else None])
    run_stage(nc, name, build)


def build_phase1(S, debug=False):
    nblk = S // 128
    nc = bass.Bass("TRN2", target_bir_lowering=False)
    x = nc.dram_tensor("x", [S, D], F32, kind="ExternalInput").ap()
    g0 = nc.dram_tensor("g0", [D], F32, kind="ExternalInput").ap()
    ident = nc.dram_tensor("ident", [128, 128], F32, kind="ExternalInput").ap()
    w1 = nc.dram_tensor("w1", [2, 128, NCH, 1024], F32, kind="ExternalInput").ap()
    tdiag = nc.dram_tensor("tdiag", [2, 128, 128], F32, kind="ExternalInput").ap()
    tadj = nc.dram_tensor("tadj", [2, 128, 128], F32, kind="ExternalInput").ap()
    mask = nc.dram_tensor("mask", [128, 128], F32, kind="ExternalInput").ap()
    c31 = nc.dram_tensor("c31", [2], F32, kind="ExternalInput").ap()
    lam = nc.dram_tensor("lam", [512], F32, kind="ExternalInput").ap()
    sg = nc.dram_tensor("sg", [256], F32, kind="ExternalInput").ap()
    ogT = nc.dram_tensor("ogT", [2, 2, 128, S], BF16, kind="ExternalOutput").ap()
    kw = {"kind": "ExternalOutput"} if debug else {}
    hT = nc.dram_tensor("hT_s", [nblk, 128, NCH, 128], BF16, **kw).ap()
    qkT = nc.dram_tensor("qkT_s", [2, 4, 128, S], BF16, **kw).ap()
    vz = nc.dram_tensor("vz_s", [2, S, 512], BF16, **kw).ap()
    stage_normT(nc, "n1", x, g0, ident, hT, nblk)
    for hh in range(2):
        stage_proj(nc, f"p{hh}", hT, w1[hh], qkT[hh], vz[hh], nblk)
        stage_attn(nc, f"a{hh}", qkT[hh], vz[hh], tdiag[hh], tadj[hh], mask, c31[hh:hh + 1], lam, sg, ident,
                   ogT[hh], nblk)
    return nc


def t5_bucket_np(n):
    n = np.asarray(n, dtype=np.int64)
    nf = np.maximum(n, 16).astype(np.float32)
    large = 16 + (np.log(nf / np.float32(16.0)) / np.float32(math.log(128 / 16)) * np.float32(16)).astype(np.int32)
    large = np.minimum(large, 31)
    return np.where(n < 16, n, large)


def phase1_inputs(x2d, norm_gain0, rel_bias, w_in, lam_p, sg, core):
    k = np.arange(128)[:, None]
    q = np.arange(128)[None, :]
    bd = t5_bucket_np(np.maximum(q - k, 0))
    ba = t5_bucket_np(q - k + 128)
    heads = [2 * core, 2 * core + 1]
    w1 = np.empty((2, 128, NCH, 1024), np.float32)
    for i, h in enumerate(heads):
        cols = np.concatenate([np.arange(h * 256, (h + 1) * 256),
                               4096 + np.arange(h * 256, (h + 1) * 256),
                               8192 + np.arange(h * 256, (h + 1) * 256),
                               12288 + np.arange(h * 256, (h + 1) * 256)])
        w1[i] = w_in[:, cols].reshape(NCH, 128, 1024).transpose(1, 0, 2)
    return {
        "x": x2d,
        "g0": np.ascontiguousarray(norm_gain0),
        "ident": np.eye(128, dtype=np.float32),
        "w1": w1,
        "tdiag": np.stack([rel_bias[bd, h] for h in heads]).astype(np.float32),
        "tadj": np.stack([rel_bias[ba, h] for h in heads]).astype(np.float32),
        "mask": np.where(q >= k, 0.0, -1e30).astype(np.float32),
        "c31": np.ascontiguousarray(rel_bias[31, heads]).astype(np.float32),
        "lam": np.ascontiguousarray(lam_p.reshape(-1)),
        "sg": np.ascontiguousarray(sg),
    }


def p2_tiles(nblk2):
    tiles = [(0, 1)]
    b = 1
    while b < nblk2:
        nb = min(4, nblk2 - b)
        tiles.append((b, nb))
        b += nb
    return tiles


def stage_wout(nc, name, og_d, wo_d, x_d, x1_d, nblk2):
    tiles = p2_tiles(nblk2)

    def build(st):
        aT = [st.sbuf(f"a{i}", [128, 4, NCH, 128], BF16) for i in range(2)]
        slab = [st.sbuf(f"w{i}", [128, NCH, 512], BF16) for i in range(2)]
        xin = [st.sbuf(f"xi{i}", [128, 512], F32) for i in range(3)]
        xo = [st.sbuf(f"xo{i}", [128, 512], F32) for i in range(3)]
        acc = [st.psum(f"acc{i}", [128, 512], F32) for i in range(2)]
        asem = [st.dsem() for _ in range(2)]
        wsem = [st.dsem() for _ in range(2)]
        xsem = [st.dsem() for _ in range(3)]
        osem = [st.dsem() for _ in range(3)]
        t_lastmm_tile = {}
        t_lastmm_slab = {}
        t_add = {}
        t_out = {}
        nslab = 0
        nacc = 0
        for ti, (b0, nb) in enumerate(tiles):
            sa = ti % 2
            t_a = None
            for b in range(nb):
                t_a = st.dma("sync", aT[sa][:, b], og_d[b0 + b], asem[sa], waits=[t_lastmm_tile.get(ti - 2)])
            for cg in range(8):
                sw = nslab % 2
                t_w = st.dma("gpsimd", slab[sw][:], wo_d[cg], wsem[sw], waits=[t_lastmm_slab.get(nslab - 2)])
                tk = None
                for b in range(nb):
                    a = nacc % 2
                    x3 = nacc % 3
                    blk = b0 + b
                    t_x = st.dma("sync", xin[x3][:], x_d[blk * 128:(blk + 1) * 128, cg * 512:(cg + 1) * 512],
                                 xsem[x3], waits=[t_add.get(nacc - 3)])
                    for c in range(NCH):
                        w = [t_a, t_w, t_add.get(nacc - 2)] if c == 0 else []
                        tk = st.op("tensor", (lambda e, a=a, c=c, b=b, sa=sa, sw=sw: e.matmul(
                            acc[a][:], lhsT=aT[sa][:, b, c, :], rhs=slab[sw][:, c, :],
                            start=(c == 0), stop=(c == NCH - 1))), waits=w, ms=(c == NCH - 1),
                            force=[t_a] if (cg == 0 and b == 0) else [])
                    t_add[nacc] = st.op("vector", (lambda e, a=a, x3=x3: e.tensor_tensor(
                        out=xo[x3][:], in0=acc[a][:], in1=xin[x3][:], op=ALU.add)),
                        waits=[tk, t_x, t_out.get(nacc - 3)], ms=True)
                    t_out[nacc] = st.dma("sync", x1_d[blk * 128:(blk + 1) * 128, cg * 512:(cg + 1) * 512], xo[x3][:],
                                         osem[x3], waits=[t_add[nacc]])
                    nacc += 1
                t_lastmm_slab[nslab] = tk
                nslab += 1
            t_lastmm_tile[ti] = tk
        st.wait_only("sync", [t_out[nacc - 1], t_out[nacc - 2], t_out[nacc - 3]])
    run_stage(nc, name, build)


def stage_pool(nc, name, h2T_d, wu_d, wz_d, wg_d, scl_d, icnt_d, mT_d, nblk2):
    tiles = p2_tiles(nblk2)

    def build(st):
        hT = [st.sbuf(f"h{i}", [128, 4, NCH, 128], BF16) for i in range(2)]
        wsl = [st.sbuf(f"w{i}", [128, NCH, 128], BF16) for i in range(3)]
        wgs = [st.sbuf(f"g{i}", [128, 16, 128], BF16) for i in range(2)]
        pg = [st.sbuf(f"pg{i}", [128, 16, 512], BF16) for i in range(2)]
        ub = [st.sbuf(f"ub{i}", [128, 528], F32) for i in range(2)]
        A = [st.sbuf(f"A{i}", [128, 528], F32) for i in range(2)]
        B = [st.sbuf(f"B{i}", [128, 528], F32) for i in range(2)]
        carry = st.sbuf("carry", [128, 64, 16], F32)
        scl = st.sbuf("scl", [128, 64], F32)
        icnt = st.sbuf("icnt", [128, 4, 16], F32)
        tmp16 = st.sbuf("tmp16", [128, 16], F32)
        ez = [st.sbuf(f"ez{i}", [128, 512], F32) for i in range(2)]
        zg = [st.sbuf(f"zg{i}", [128, 512], F32) for i in range(2)]
        mst = [st.sbuf(f"ms{i}", [128, 512], BF16) for i in range(2)]
        accu = [st.psum(f"au{i}", [128, 4, 128], F32) for i in range(2)]
        accm = [st.psum(f"am{i}", [128, 512], F32) for i in range(2)]
        accz = [st.psum(f"az{i}", [128, 4, 128], F32) for i in range(2)]
        hsem = [st.dsem() for _ in range(2)]
        wsem = [st.dsem() for _ in range(3)]
        gsem = [st.dsem() for _ in range(2)]
        osem = [st.dsem() for _ in range(2)]
        cs = st.dsem()
        st.dma("sync", scl[:], scl_d, cs)
        t_c = st.dma("sync", icnt[:].rearrange("p a b -> p (a b)"),
                     bass.AP(icnt_d.tensor, icnt_d.offset, [[0, 128], [1, 64]]), cs)
        cnt = {"w": 0, "u": 0, "g": 0, "m": 0, "pg": 0}
        t_wfree = {}
        t_gfree = {}
        t_u_ev = {}
        t_pool_done = {}
        t_pg_last_mm = {}
        t_m_fin = {}
        t_m_out = {}
        t_tile_last_mm = {}
        t_carry = {}

        def load_w(src):
            k = cnt["w"]
            cnt["w"] += 1
            s = k % 3
            tok = st.dma("gpsimd", wsl[s][:], src, wsem[s], waits=[t_wfree.get(k - 3)])
            return k, s, tok

        for ti, (b0, nb) in enumerate(tiles):
            T = nb * 128
            halo = ti == 0
            sh = ti % 2
            t_h = None
            for b in range(nb):
                t_h = st.dma("sync", hT[sh][:, b], h2T_d[b0 + b], hsem[sh], waits=[t_tile_last_mm.get(ti - 2)])
            last_mm = None
            for g in range(4):
                wwin = WINDOWS[g]
                ipg = cnt["pg"]
                cnt["pg"] += 1
                pgb = pg[ipg % 2]
                t_p_list = []
                for cc in range(16):
                    oc = g * 16 + cc
                    kw, sw, t_w = load_w(wu_d[oc])
                    nu = cnt["u"]
                    cnt["u"] += 1
                    a = nu % 2
                    tk = None
                    for c in range(NCH):
                        w = [t_h, t_w, t_u_ev.get(nu - 2)] if c == 0 else []
                        tk = st.op("tensor", (lambda e, a=a, c=c, sw=sw, sh=sh, nb=nb: e.matmul(
                            accu[a][:, 0:nb, :], lhsT=wsl[sw][:, c, :], rhs=hT[sh][:, 0:nb, c, :],
                            start=(c == 0), stop=(c == NCH - 1))), waits=w, ms=(c == NCH - 1), force=[t_w])
                    t_wfree[kw] = tk
                    last_mm = tk
                    U = ub[a]
                    e1 = st.op("scalar", (lambda e, a=a, U=U, nb=nb, T=T: e.activation(
                        out=U[:, 16:16 + T], in_=accu[a][:, 0:nb, :].rearrange("p a b -> p (a b)"), func=AF.Copy)),
                        waits=[tk, t_pool_done.get(nu - 2)], ms=True)
                    t_u_ev[nu] = e1
                    if halo:
                        c1 = st.op("vector", (lambda e, U=U, oc=oc, T=T: e.tensor_copy(
                            out=carry[:, oc, :], in_=U[:, T:T + 16])), waits=[e1], ms=True)
                        t_carry[oc] = c1
                        t_pool_done[nu] = c1
                        continue
                    c0 = st.op("vector", (lambda e, U=U, oc=oc: e.tensor_copy(out=U[:, 0:16], in_=carry[:, oc, :])),
                               waits=[t_carry.get(oc), t_pool_done.get(nu - 2)], ms=True)
                    c1 = st.op("vector", (lambda e, U=U, oc=oc, T=T: e.tensor_copy(
                        out=carry[:, oc, :], in_=U[:, T:T + 16])), waits=[e1, c0], ms=True)
                    t_carry[oc] = c1
                    L = 16 + T
                    Aa, Bb = A[a], B[a]
                    p1 = st.op("vector", (lambda e, U=U, Aa=Aa, L=L: e.tensor_tensor(
                        out=Aa[:, 1:L], in0=U[:, 1:L], in1=U[:, 0:L - 1], op=ALU.add)), waits=[c1], ms=True)
                    sw_buf = Aa
                    prev = p1
                    if wwin >= 4:
                        prev = st.op("vector", (lambda e, Aa=Aa, Bb=Bb, L=L: e.tensor_tensor(
                            out=Bb[:, 3:L], in0=Aa[:, 3:L], in1=Aa[:, 1:L - 2], op=ALU.add)), waits=[prev], ms=True)
                        sw_buf = Bb
                    if wwin >= 8:
                        prev = st.op("vector", (lambda e, Aa=Aa, Bb=Bb, L=L: e.tensor_tensor(
                            out=Aa[:, 7:L], in0=Bb[:, 7:L], in1=Bb[:, 3:L - 4], op=ALU.add)), waits=[prev], ms=True)
                        sw_buf = Aa
                    if wwin >= 16:
                        prev = st.op("vector", (lambda e, Aa=Aa, Bb=Bb, L=L: e.tensor_tensor(
                            out=Bb[:, 15:L], in0=Aa[:, 15:L], in1=Aa[:, 7:L - 8], op=ALU.add)), waits=[prev], ms=True)
                        sw_buf = Bb
                    pw = [prev, t_pg_last_mm.get(ipg - 2)]
                    p2 = st.op("vector", (lambda e, U=U, sb_=sw_buf, cc=cc, T=T, pgb=pgb, wwin=wwin:
                                          e.scalar_tensor_tensor(out=pgb[:, cc, 0:T], in0=sb_[:, 16:16 + T],
                                                                 scalar=1.0 / wwin, in1=U[:, 16:16 + T],
                                                                 op0=ALU.mult, op1=ALU.subtract)),
                               waits=pw, ms=True)
                    if ti == 1:
                        q1 = st.op("vector", (lambda e, sb_=sw_buf, g=g: e.tensor_tensor(
                            out=tmp16[:], in0=sb_[:, 16:32], in1=icnt[:, g, :], op=ALU.mult)),
                            waits=[p2, t_c], ms=True)
                        p2 = st.op("vector", (lambda e, U=U, cc=cc, pgb=pgb: e.tensor_tensor(
                            out=pgb[:, cc, 0:16], in0=tmp16[:], in1=U[:, 16:32], op=ALU.subtract)),
                            waits=[q1], ms=True)
                    t_pool_done[nu] = p2
                    t_p_list.append(p2)
                if halo:
                    continue
                t_pg_ready = t_p_list[-1]
                for dc in range(16):
                    e_ = g * 16 + dc
                    kg = cnt["g"]
                    cnt["g"] += 1
                    sg_ = kg % 2
                    t_g = st.dma("gpsimd", wgs[sg_][:], wg_d[g, dc], gsem[sg_], waits=[t_gfree.get(kg - 2)])
                    kw, sw, t_w = load_w(wz_d[e_])
                    nm = cnt["m"]
                    cnt["m"] += 1
                    a = nm % 2
                    tk = None
                    for cc in range(16):
                        w = [t_g, t_pg_ready, t_m_fin.get(nm - 2)] if cc == 0 else []
                        tk = st.op("tensor", (lambda e, a=a, cc=cc, sg_=sg_, pgb=pgb, T=T: e.matmul(
                            accm[a][:, 0:T], lhsT=wgs[sg_][:, cc, :], rhs=pgb[:, cc, 0:T],
                            start=(cc == 0), stop=(cc == 15))), waits=w, ms=(cc == 15), force=[t_g])
                    t_gfree[kg] = tk
                    t_mm_m = tk
                    if dc == 15:
                        t_pg_last_mm[ipg] = tk
                    for c in range(NCH):
                        w = [t_w] if c == 0 else []
                        tk = st.op("tensor", (lambda e, a=a, c=c, sw=sw, sh=sh, nb=nb: e.matmul(
                            accz[a][:, 0:nb, :], lhsT=wsl[sw][:, c, :], rhs=hT[sh][:, 0:nb, c, :],
                            start=(c == 0), stop=(c == NCH - 1))), waits=w, ms=(c == NCH - 1), force=[t_w])
                    t_wfree[kw] = tk
                    last_mm = tk
                    zz = (lambda a=a, nb=nb: accz[a][:, 0:nb, :].rearrange("p a b -> p (a b)"))
                    h1 = st.op("scalar", (lambda e, a=a, T=T, zz=zz: e.activation(
                        out=ez[a][:, 0:T], in_=zz(), func=AF.Exp, scale=-1.0)),
                        waits=[tk, t_m_fin.get(nm - 2)], ms=True)
                    h2 = st.op("gpsimd", (lambda e, a=a, T=T: e.tensor_scalar(
                        out=ez[a][:, 0:T], in0=ez[a][:, 0:T], scalar1=1.0, scalar2=1.0, op0=ALU.add, op1=ALU.mult)),
                        waits=[h1], ms=True)
                    h3 = st.op("vector", (lambda e, a=a, T=T: e.reciprocal(out=ez[a][:, 0:T], in_=ez[a][:, 0:T])),
                               waits=[h2], ms=True)
                    h4 = st.op("vector", (lambda e, a=a, T=T, zz=zz: e.tensor_tensor(
                        out=zg[a][:, 0:T], in0=zz(), in1=ez[a][:, 0:T], op=ALU.mult)), waits=[h3], ms=True)
                    h5 = st.op("vector", (lambda e, a=a, T=T, e_=e_: e.scalar_tensor_tensor(
                        out=mst[a][:, 0:T], in0=accm[a][:, 0:T], scalar=scl[:, e_:e_ + 1], in1=zg[a][:, 0:T],
                        op0=ALU.mult, op1=ALU.mult)), waits=[h4, t_mm_m, t_m_out.get(nm - 2), t_c], ms=True)
                    t_m_fin[nm] = h5
                    t_m_out[nm] = st.dma("sync", mT_d[ti - 1, :, e_, 0:T], mst[a][:, 0:T], osem[a], waits=[h5])
            t_tile_last_mm[ti] = last_mm
        nm = cnt["m"]
        st.wait_only("sync", [t_m_out.get(nm - 1), t_m_out.get(nm - 2)])
    run_stage(nc, name, build)


def stage_out(nc, name, mT_d, wo2_d, x1_d, out_d, ntile):
    def build(st):
        mT = st.sbuf("mT", [128, 64, 512], BF16)
        x2 = st.sbuf("x2", [128, 4, D], F32)
        slab = [st.sbuf(f"w{i}", [128, 64, 256], BF16) for i in range(2)]
        acc = [st.psum(f"acc{i}", [128, 256], F32) for i in range(4)]
        msem = st.dsem()
        xsem = st.dsem()
        wsem = [st.dsem() for _ in range(2)]
        osem = st.dsem()
        t_slab_free = {}
        t_add = {}
        nslab = 0
        nacc = 0
        t_tile_mm = None
        t_tile_add = None
        t_o = None
        for ti in range(ntile):
            t_m = None
            for h in range(4):
                t_m = st.dma("sync", mT[:, h * 16:(h + 1) * 16, :], mT_d[ti, :, h * 16:(h + 1) * 16, :], msem,
                             waits=[t_tile_mm])
            t_x = None
            for b in range(4):
                r0 = 128 + (ti * 4 + b) * 128
                t_x = st.dma("sync", x2[:, b, :], x1_d[r0:r0 + 128, :], xsem, waits=[t_o])
            for cg in range(16):
                sw = nslab % 2
                t_w = None
                for h in range(2):
                    t_w = st.dma("gpsimd", slab[sw][:, h * 32:(h + 1) * 32, :], wo2_d[cg, :, h * 32:(h + 1) * 32, :],
                                 wsem[sw], waits=[t_slab_free.get(nslab - 2)])
                tk = None
                for b in range(4):
                    a = nacc % 4
                    for e_ in range(64):
                        w = [t_m, t_w, t_add.get(nacc - 4)] if e_ == 0 else []
                        tk = st.op("tensor", (lambda e, a=a, e_=e_, b=b, sw=sw: e.matmul(
                            acc[a][:], lhsT=mT[:, e_, b * 128:(b + 1) * 128], rhs=slab[sw][:, e_, :],
                            start=(e_ == 0), stop=(e_ == 63))), waits=w, ms=(e_ == 63),
                            force=[t_m] if (cg == 0 and b == 0) else [])
                    t_add[nacc] = st.op("vector", (lambda e, a=a, b=b, cg=cg: e.tensor_tensor(
                        out=x2[:, b, cg * 256:(cg + 1) * 256], in0=acc[a][:], in1=x2[:, b, cg * 256:(cg + 1) * 256],
                        op=ALU.add)), waits=[tk, t_x], ms=True)
                    t_tile_add = t_add[nacc]
                    nacc += 1
                t_slab_free[nslab] = tk
                nslab += 1
            t_tile_mm = tk
            for b in range(4):
                r0 = (ti * 4 + b) * 128
                t_o = st.dma("sync", out_d[r0:r0 + 128, :], x2[:, b, :], osem, waits=[t_tile_add])
        st.wait_only("sync", [t_o])
    run_stage(nc, name, build)


def stage_fnorm(nc, name, out_d, gain_d, nblk):
    def build(st):
        xb = [st.sbuf(f"x{i}", [128, D], F32) for i in range(2)]
        yb = [st.sbuf(f"y{i}", [128, D], F32) for i in range(2)]
        gb = st.sbuf("gb", [128, D], F32)
        ss = st.sbuf("ss", [128, 2], F32)
        rs = st.sbuf("rs", [128, 2], F32)
        epsb = st.sbuf("epsb", [128, 1], F32)
        xs = [st.dsem() for _ in range(2)]
        os_ = [st.dsem() for _ in range(2)]
        cs = st.dsem()
        t_eps = st.op("vector", lambda e: e.memset(epsb[:], D * NORM_EPS), ms=True)
        t_gb0 = st.dma("gpsimd", gb[:], bass.AP(gain_d.tensor, gain_d.offset, [[0, 128], [1, D]]), cs)
        t_gb = st.op("vector", lambda e: e.tensor_scalar(out=gb[:], in0=gb[:], scalar1=float(D ** 0.5), scalar2=0.0,
                                                         op0=ALU.mult, op1=ALU.add), waits=[t_gb0], ms=True)
        tY = [None] * nblk
        tO = [None] * nblk
        for i in range(nblk):
            s = i % 2
            t_x = st.dma("sync", xb[s][:], out_d[i * 128:(i + 1) * 128, :], xs[s], waits=[tY[i - 2] if i >= 2 else None])
            t0 = st.op("vector", (lambda e, s=s: e.memset(ss[:, s:s + 1], 0.0)), ms=True,
                       waits=[tY[i - 2] if i >= 2 else None])
            tA = st.op("scalar", (lambda e, s=s: e.activation(out=yb[s][:], in_=xb[s][:], func=AF.Square,
                                                              accum_out=ss[:, s:s + 1])),
                       waits=[t_x, t0, tO[i - 2] if i >= 2 else None], ms=True)
            t1 = st.op("scalar", (lambda e, s=s: e.activation(out=rs[:, s:s + 1], in_=ss[:, s:s + 1], func=AF.Sqrt,
                                                              bias=epsb[:, 0:1], scale=1.0)), waits=[tA, t_eps], ms=True)
            t2 = st.op("vector", (lambda e, s=s: e.reciprocal(out=rs[:, s:s + 1], in_=rs[:, s:s + 1])),
                       waits=[t1], ms=True)
            tY[i] = st.op("vector", (lambda e, s=s: e.scalar_tensor_tensor(out=yb[s][:], in0=xb[s][:],
                                                                           scalar=rs[:, s:s + 1], in1=gb[:],
                                                                           op0=ALU.mult, op1=ALU.mult)),
                          waits=[t2, t_gb], ms=True)
            tO[i] = st.dma("sync", out_d[i * 128:(i + 1) * 128, :], yb[s][:], os_[s], waits=[tY[i]])
        st.wait_only("sync", [tO[nblk - 1], tO[nblk - 2] if nblk >= 2 else None])
    run_stage(nc, name, build)


def build_phase2(nblk2, debug=False):
    nmain = nblk2 - 1
    ntile = nmain // 4
    nc = bass.Bass("TRN2", target_bir_lowering=False)
    og = nc.dram_tensor("og", [nblk2, 128, NCH, 128], BF16, kind="ExternalInput").ap()
    xs = nc.dram_tensor("xs", [nblk2 * 128, D], F32, kind="ExternalInput").ap()
    ident = nc.dram_tensor("ident", [128, 128], F32, kind="ExternalInput").ap()
    g1 = nc.dram_tensor("g1", [D], F32, kind="ExternalInput").ap()
    gf = nc.dram_tensor("gf", [D], F32, kind="ExternalInput").ap()
    wo = nc.dram_tensor("wo", [8, 128, NCH, 512], F32, kind="ExternalInput").ap()
    wu = nc.dram_tensor("wu", [64, 128, NCH, 128], F32, kind="ExternalInput").ap()
    wz = nc.dram_tensor("wz", [64, 128, NCH, 128], F32, kind="ExternalInput").ap()
    wg = nc.dram_tensor("wg", [4, 16, 128, 16, 128], F32, kind="ExternalInput").ap()
    scl = nc.dram_tensor("scl", [128, 64], F32, kind="ExternalInput").ap()
    icnt = nc.dram_tensor("icnt", [64], F32, kind="ExternalInput").ap()
    wo2 = nc.dram_tensor("wo2", [16, 128, 64, 256], F32, kind="ExternalInput").ap()
    out = nc.dram_tensor("out", [nmain * 128, D], F32, kind="ExternalOutput").ap()
    kw = {"kind": "ExternalOutput"} if debug else {}
    x1 = nc.dram_tensor("x1_s", [nblk2 * 128, D], F32, **kw).ap()
    h2T = nc.dram_tensor("h2T_s", [nblk2, 128, NCH, 128], BF16, **kw).ap()
    mT = nc.dram_tensor("mT_s", [ntile, 128, 64, 512], BF16, **kw).ap()
    stage_wout(nc, "wo", og, wo, xs, x1, nblk2)
    stage_normT(nc, "n2", x1, g1, ident, h2T, nblk2)
    stage_pool(nc, "pl", h2T, wu, wz, wg, scl, icnt, mT, nblk2)
    stage_out(nc, "ou", mT, wo2, x1, out, ntile)
    stage_fnorm(nc, "fn", out, gf, nmain)
    return nc


def phase2_weights(w_out, pool_w_in, pool_w_group, pool_scale, pool_w_out, g1, gf):
    return {
        "ident": np.eye(128, dtype=np.float32),
        "g1": np.ascontiguousarray(g1),
        "gf": np.ascontiguousarray(gf),
        "wo": np.ascontiguousarray(w_out.reshape(NCH, 128, 8, 512).transpose(2, 1, 0, 3)),
        "wu": np.ascontiguousarray(pool_w_in[:, :PW].reshape(NCH, 128, 64, 128).transpose(2, 1, 0, 3)),
        "wz": np.ascontiguousarray(pool_w_in[:, PW:].reshape(NCH, 128, 64, 128).transpose(2, 1, 0, 3)),
        "wg": np.ascontiguousarray(pool_w_group.reshape(4, 16, 128, 16, 128).transpose(0, 3, 2, 1, 4)),
        "scl": np.ascontiguousarray(pool_scale.reshape(64, 128).T),
        "wo2": np.ascontiguousarray(pool_w_out.reshape(64, 128, 16, 256).transpose(2, 1, 0, 3)),
    }


def phase2_core_inputs(x2d, ogT_all, core, ntok, first):
    nblk2 = ntok // 128 + 1
    t0 = core * ntok - 128
    xs = np.zeros((nblk2 * 128, D), np.float32)
    og = np.zeros((D, nblk2 * 128), ml_dtypes.bfloat16)
    lo = max(t0, 0)
    xs[lo - t0:] = x2d[lo:(core + 1) * ntok]
    og[:, lo - t0:] = ogT_all[:, lo:(core + 1) * ntok]
    og = np.ascontiguousarray(og.reshape(NCH, 128, nblk2, 128).transpose(2, 1, 0, 3))
    icnt = np.empty((4, 16), np.float32)
    for g, w in enumerate(WINDOWS):
        if first:
            icnt[g] = 1.0 / np.minimum(np.arange(16) + 1, w)
        else:
            icnt[g] = 1.0 / w
    return {"xs": xs, "og": og, "icnt": icnt.reshape(-1)}


def kernel(x, norm_gains, final_norm_gain, rel_bias, attn_w_in, attn_lambda, attn_subln_gain,
           attn_w_out, pool_w_in, pool_w_group, pool_scale, pool_w_out):
    f32 = np.float32
    x2d = np.ascontiguousarray(np.asarray(x, f32)[0])
    S = x2d.shape[0]
    ntok = S // NCORES
    norm_gains = np.asarray(norm_gains, f32)
    rel_bias = np.asarray(rel_bias, f32)
    nc1 = build_phase1(S)
    w_in = np.asarray(attn_w_in, f32)[0]
    ins1 = [phase1_inputs(x2d, norm_gains[0], rel_bias, w_in, np.asarray(attn_lambda, f32)[0],
                          np.asarray(attn_subln_gain, f32)[0], c) for c in range(NCORES)]
    res1 = run_bass_kernel_spmd(nc1, ins1, core_ids=list(range(NCORES)))
    ogT_all = np.concatenate([np.asarray(res1.results[c]["ogT"]).reshape(512, S) for c in range(NCORES)], axis=0)
    del ins1, res1
    nc2 = build_phase2(ntok // 128 + 1)
    W = phase2_weights(np.asarray(attn_w_out, f32)[0], np.asarray(pool_w_in, f32)[0],
                       np.asarray(pool_w_group, f32)[0], np.asarray(pool_scale, f32)[0],
                       np.asarray(pool_w_out, f32)[0], norm_gains[1], np.asarray(final_norm_gain, f32))
    ins2 = []
    for c in range(NCORES):
        d = dict(W)
        d.update(phase2_core_inputs(x2d, ogT_all, c, ntok, c == 0))
        ins2.append(d)
    res2 = run_bass_kernel_spmd(nc2, ins2, core_ids=list(range(NCORES)))
    out = np.concatenate([np.asarray(res2.results[c]["out"], dtype=f32) for c in range(NCORES)], axis=0)
    return out.reshape(1, S, D)
```
